# Optimizing a Trainium2 kernel written in Bass

```python
import math
import jax, jax.numpy as jnp
from jax import lax
import numpy as np

D_MODEL = 1024
BATCH = 16
SEQ = 4096
DEPTH = 4

N_MIXERS = 2
PLE_DIM = 256
FFN_HIDDEN = ((8 * D_MODEL // 3 + 255) // 256) * 256
DA_HEADS = 8
DA_HEAD_DIM = D_MODEL // (2 * DA_HEADS)
Q_BLOCK = 128
NA_HEADS = 16
NA_HEAD_DIM = D_MODEL // NA_HEADS
GRID_W = 64
NA_KR = 8
NA_KC = 16
NORM_EPS = 1e-6
SUBLN_EPS = 1e-5

kernel_name = "hybrid_diffattn_natten_encoder"


def rmsnorm(x, g, eps=NORM_EPS):
    xf = x.astype(jnp.float32)
    y = xf * lax.rsqrt(jnp.mean(xf * xf, axis=-1, keepdims=True) + eps)
    return (y * g.astype(jnp.float32)).astype(x.dtype)


def alibi_slopes(n_heads):
    return jnp.array([2.0 ** (-8.0 * (h + 1) / n_heads) for h in range(n_heads)], dtype=jnp.float32)


def diff_attention(h, w_qkv, w_o, lq1, lk1, lq2, lk2, subln, layer_idx):
    B, S, _ = h.shape
    qkv = h @ w_qkv
    q, k, v = jnp.split(qkv, 3, axis=-1)
    q = q.reshape(B, S, DA_HEADS, 2, DA_HEAD_DIM).transpose(3, 0, 2, 1, 4)
    k = k.reshape(B, S, DA_HEADS, 2, DA_HEAD_DIM).transpose(3, 0, 2, 1, 4)
    v = v.reshape(B, S, DA_HEADS, 2 * DA_HEAD_DIM).transpose(0, 2, 1, 3)
    lam_init = 0.8 - 0.6 * math.exp(-0.3 * layer_idx)
    lam = (jnp.exp(jnp.sum(lq1.astype(jnp.float32) * lk1.astype(jnp.float32)))
           - jnp.exp(jnp.sum(lq2.astype(jnp.float32) * lk2.astype(jnp.float32)))
           + lam_init)
    slopes = alibi_slopes(DA_HEADS)
    scale = DA_HEAD_DIM ** -0.5
    nblk = S // Q_BLOCK
    qb = q.reshape(2, B, DA_HEADS, nblk, Q_BLOCK, DA_HEAD_DIM).transpose(3, 0, 1, 2, 4, 5)
    kpos = jnp.arange(S)

    def block(args):
        qblk, bi = args
        qpos = bi * Q_BLOCK + jnp.arange(Q_BLOCK)
        dist = jnp.abs(qpos[:, None] - kpos[None, :]).astype(jnp.float32)
        bias = -slopes[:, None, None] * dist
        s = jnp.einsum('mbhqd,mbhkd->mbhqk', qblk, k).astype(jnp.float32) * scale + bias
        a = jax.nn.softmax(s, axis=-1)
        a = a[0] - lam * a[1]
        return jnp.einsum('bhqk,bhkd->bhqd', a.astype(v.dtype), v)

    o = lax.map(block, (qb, jnp.arange(nblk)))
    o = o.transpose(1, 0, 3, 2, 4).reshape(B, S, DA_HEADS, 2 * DA_HEAD_DIM)
    o = rmsnorm(o, subln, eps=SUBLN_EPS) * (1.0 - lam_init)
    return o.reshape(B, S, D_MODEL) @ w_o


def neighborhood_attention(h, w_qkv, b_qkv, rpb, w_o):
    B, S, _ = h.shape
    rows = S // GRID_W
    kr = min(NA_KR, rows)
    W = GRID_W
    qkv = h @ w_qkv + b_qkv
    qkv = qkv.reshape(B, rows, W, 3, NA_HEADS, NA_HEAD_DIM).transpose(3, 0, 4, 1, 2, 5)
    q = qkv[0] * (NA_HEAD_DIM ** -0.5)
    k = qkv[1]
    v = qkv[2]
    cols = jnp.arange(W)
    cstart = jnp.clip(cols - NA_KC // 2, 0, W - NA_KC)
    col_mask = (cols[None, :] >= cstart[:, None]) & (cols[None, :] < cstart[:, None] + NA_KC)
    col_off = jnp.clip(cols[None, :] - cols[:, None], -(NA_KC - 1), NA_KC - 1) + (NA_KC - 1)
    rpb_cols = rpb[:, :, col_off]
    key_mask = jnp.tile(col_mask, (1, kr))

    def row_block(r):
        rs = jnp.clip(r - kr // 2, 0, rows - kr)
        kb = lax.dynamic_slice_in_dim(k, rs, kr, axis=2).reshape(B, NA_HEADS, kr * W, NA_HEAD_DIM)
        vb = lax.dynamic_slice_in_dim(v, rs, kr, axis=2).reshape(B, NA_HEADS, kr * W, NA_HEAD_DIM)
        qr = lax.dynamic_index_in_dim(q, r, axis=2, keepdims=False)
        row_off = rs + jnp.arange(kr) - r + (NA_KR - 1)
        bias = jnp.take(rpb_cols, row_off, axis=1)
        bias = bias.transpose(0, 2, 1, 3).reshape(NA_HEADS, W, kr * W).astype(jnp.float32)
        s = jnp.einsum('bhqd,bhkd->bhqk', qr, kb).astype(jnp.float32) + bias
        s = jnp.where(key_mask, s, -jnp.inf)
        a = jax.nn.softmax(s, axis=-1)
        return jnp.einsum('bhqk,bhkd->bhqd', a.astype(vb.dtype), vb)

    o = lax.map(row_block, jnp.arange(rows))
    o = o.transpose(1, 0, 3, 2, 4).reshape(B, S, D_MODEL)
    return o @ w_o


def swiglu(h, w_gate, w_up, w_down):
    return (jax.nn.silu(h @ w_gate) * (h @ w_up)) @ w_down


def setup_inputs(seed: int = 0) -> dict:
    key = jax.random.key(seed)
    ks = jax.random.split(key, 24)
    n_a = (DEPTH + 1) // 2
    n_b = DEPTH // 2
    D, F = D_MODEL, FFN_HIDDEN
    f32 = jnp.float32

    def nrm(k, shape, scale):
        return jax.random.normal(k, shape, f32) * scale

    return {
        "x": nrm(ks[0], (BATCH, SEQ, D), 1.0),
        "p": nrm(ks[1], (DEPTH, BATCH, SEQ, PLE_DIM), 1.0),
        "norm_mix": 1.0 + nrm(ks[2], (DEPTH, D), 0.05),
        "norm_ffn": 1.0 + nrm(ks[3], (DEPTH, D), 0.05),
        "norm_ple": 1.0 + nrm(ks[4], (DEPTH, D), 0.05),
        "norm_final": 1.0 + nrm(ks[5], (D,), 0.05),
        "da_w_qkv": nrm(ks[6], (n_a, D, 3 * D), D ** -0.5),
        "da_w_o": nrm(ks[7], (n_a, D, D), D ** -0.5),
        "da_lambda_q1": nrm(ks[8], (n_a, DA_HEAD_DIM), 0.1),
        "da_lambda_k1": nrm(ks[9], (n_a, DA_HEAD_DIM), 0.1),
        "da_lambda_q2": nrm(ks[10], (n_a, DA_HEAD_DIM), 0.1),
        "da_lambda_k2": nrm(ks[11], (n_a, DA_HEAD_DIM), 0.1),
        "da_subln": 1.0 + nrm(ks[12], (n_a, 2 * DA_HEAD_DIM), 0.05),
        "na_w_qkv": nrm(ks[13], (n_b, D, 3 * D), D ** -0.5),
        "na_b_qkv": nrm(ks[14], (n_b, 3 * D), 0.02),
        "na_rpb": nrm(ks[15], (n_b, NA_HEADS, 2 * NA_KR - 1, 2 * NA_KC - 1), 0.5),
        "na_w_o": nrm(ks[16], (n_b, D, D), D ** -0.5),
        "ffn_w_gate": nrm(ks[17], (DEPTH, D, F), D ** -0.5),
        "ffn_w_up": nrm(ks[18], (DEPTH, D, F), D ** -0.5),
        "ffn_w_down": nrm(ks[19], (DEPTH, F, D), F ** -0.5),
        "ple_w_proj": nrm(ks[20], (DEPTH, PLE_DIM, D), PLE_DIM ** -0.5),
        "ple_w_gate": nrm(ks[21], (DEPTH, D, D), D ** -0.5),
    }


def reference(x, p, norm_mix, norm_ffn, norm_ple, norm_final,
              da_w_qkv, da_w_o, da_lambda_q1, da_lambda_k1, da_lambda_q2, da_lambda_k2, da_subln,
              na_w_qkv, na_b_qkv, na_rpb, na_w_o,
              ffn_w_gate, ffn_w_up, ffn_w_down, ple_w_proj, ple_w_gate):
    h = x
    for i in range(DEPTH):
        hn = rmsnorm(h, norm_mix[i])
        j = i // N_MIXERS
        if i % N_MIXERS == 0:
            mix = diff_attention(hn, da_w_qkv[j], da_w_o[j], da_lambda_q1[j], da_lambda_k1[j],
                                 da_lambda_q2[j], da_lambda_k2[j], da_subln[j], i)
        else:
            mix = neighborhood_attention(hn, na_w_qkv[j], na_b_qkv[j], na_rpb[j], na_w_o[j])
        h = h + mix
        h = h + swiglu(rmsnorm(h, norm_ffn[i]), ffn_w_gate[i], ffn_w_up[i], ffn_w_down[i])
        gate = jax.nn.sigmoid(rmsnorm(h, norm_ple[i]) @ ple_w_gate[i])
        h = h + gate * (p[i] @ ple_w_proj[i])
    return rmsnorm(h, norm_final)
```

```python
import math
from contextlib import ExitStack

import numpy as np
import ml_dtypes

import concourse.bass as bass
import concourse.mybir as mybir
from concourse.bass_utils import run_bass_kernel_spmd

F32 = mybir.dt.float32
BF16 = mybir.dt.bfloat16
AF = mybir.ActivationFunctionType
ALU = mybir.AluOpType
AX = mybir.AxisListType

D = 1024
S = 4096
NSEQ = 2
DEPTH = 4
FF = 2816
FC = FF // 128
PLE = 256
TB = 512
NBLK = S // TB
NCORES = 8
MASKV = -30000.0
SEM_LIMIT = 30000


class Buf:
    __slots__ = ("name", "w", "r", "dsem")

    def __init__(self, name=""):
        self.name = name
        self.w = {}
        self.r = {}
        self.dsem = {}


def _merge(dst, src):
    for k, v in src.items():
        if dst.get(k, 0) < v:
            dst[k] = v


class Sync:
    def __init__(self, nc, stack):
        self.nc = nc
        self.stack = stack
        self.eng = {"pe": nc.tensor, "act": nc.scalar, "dve": nc.vector,
                    "pool": nc.gpsimd, "sp": nc.sync}
        self.cnt = {}
        self.esem = {}
        self.known = {e: {} for e in self.eng}
        self.nsem = 0
        self.log = None
        self.free = {}
        self.dma_owners = []
        self.pending = {e: 0 for e in self.eng}
        for e in ("pe", "act", "dve", "pool"):
            self.esem[e] = self._new_sem(e)

    def _new_sem(self, name):
        self.nsem += 1
        h = self.stack.enter_context(self.nc.semaphore(f"{name}_{self.nsem}"))
        self.cnt[h] = 0
        return h

    def _wait(self, e, needs):
        k = self.known[e]
        own = self.esem.get(e)
        for s, v in needs.items():
            if k.get(s, 0) >= v:
                continue
            if e == "pe" and s is own:
                continue
            self.eng[e].wait_ge(s, v)
            if self.log is not None:
                self.log[e].append(("w", s.name, v))
            k[s] = v

    def _needs(self, reads, writes):
        needs = {}
        for b in reads:
            _merge(needs, b.w)
        for b in writes:
            _merge(needs, b.w)
            _merge(needs, b.r)
        return needs

    def _record(self, s, v, reads, writes):
        for b in reads:
            if b.r.get(s, 0) < v:
                b.r[s] = v
        for b in writes:
            b.w = {s: v}
            b.r = {}

    def op(self, e, fn, reads=(), writes=(), signal=True):
        self._wait(e, self._needs(reads, writes))
        s = self.esem[e]
        if self.cnt[s] >= SEM_LIMIT and self.pending[e] == 0:
            s = self.esem[e] = self._new_sem(e)
        v = self.cnt[s] + 1
        if self.log is not None:
            self.log[e].append(("i", s.name, 1 if signal else 0))
        if signal:
            self.cnt[s] = v
            fn().then_inc(s, 1)
            self.pending[e] = 0
        else:
            fn()
            self.pending[e] += 1
        self._record(s, v, reads, writes)

    def dma(self, q, out_ap, in_ap, owner, reads=(), writes=(), slow=False):
        self._wait(q, self._needs(reads, writes))
        s = owner.dsem.get(q)
        if s is not None and self.cnt[s] + 16 > SEM_LIMIT:
            s = None
        if s is None:
            fl = self.free.setdefault(q, [])
            while fl:
                c = fl.pop()
                if self.cnt[c] + 16 <= SEM_LIMIT:
                    s = c
                    break
            if s is None:
                s = self._new_sem("d" + q)
            owner.dsem[q] = s
            self.dma_owners.append(owner)
        self.cnt[s] += 16
        v = self.cnt[s]
        if self.log is not None:
            self.log[q].append(("i", s.name, 16))
        if slow:
            self.eng[q].dma_start(out=out_ap, in_=in_ap, allow_slow_non_contiguous=True).then_inc(s, 16)
        else:
            self.eng[q].dma_start(out=out_ap, in_=in_ap).then_inc(s, 16)
        self._record(s, v, reads, writes)

    def barrier(self):
        assert all(v == 0 for v in self.pending.values()), self.pending
        allev = {s: c for s, c in self.cnt.items() if c > 0}
        for e in self.eng:
            self._wait(e, allev)
        for b in self.dma_owners:
            for q, s in b.dsem.items():
                self.free.setdefault(q, []).append(s)
            b.dsem = {}
        self.dma_owners = []


def _const_tables():
    tok = np.arange(S)
    qaug = np.stack([tok % 256, tok // 256, np.ones(S), np.ones(S)]).astype(np.float32)
    kaug = np.zeros((8, 2, 4, S), np.float32)
    for h in range(8):
        sl = 2.0 ** (-(h + 1))
        L = np.stack([np.full(S, -sl), np.full(S, -256.0 * sl), sl * (tok % 256), 256.0 * sl * (tok // 256)])
        kaug[h, 0] = L
        kaug[h, 1] = -L
    pp = np.arange(128)[:, None]
    ff = np.arange(512)[None, :]
    cbase = np.stack([np.maximum(dd * 128 + pp - ff, 0) for dd in range(4)]).astype(np.float32)
    return (qaug.astype(ml_dtypes.bfloat16), kaug.astype(ml_dtypes.bfloat16), cbase)


def _na_bias_table(rpb):
    W = 64
    cols = np.arange(W)
    cstart = np.clip(cols - 8, 0, W - 16)
    wk = cols[:, None]
    wq = cols[None, :]
    cmask = (wk >= cstart[None, :]) & (wk < cstart[None, :] + 16)
    coff = np.clip(wk - wq, -15, 15) + 15
    g = rpb[:, ::-1, :][:, :, coff]
    g = np.where(cmask[None, None], g, np.float32(MASKV))
    return np.ascontiguousarray(g.transpose(0, 2, 1, 3)).astype(np.float32)


def _na_rs(r):
    return min(max(r - 4, 0), 56)


def _na_tiles(b):
    return list(range(max(0, 4 * b - 2), min(31, 4 * b + 5) + 1))


def _na_btype(b):
    return 0 if b == 0 else (2 if b == 7 else 1)


def _na_valid_runs(b, t):
    out = []
    for a in range(2):
        rk = 2 * t + a
        cs = [c for c in range(8) if _na_rs(8 * b + c) <= rk <= _na_rs(8 * b + c) + 7]
        if not cs:
            out.append(None)
            continue
        c0, c1 = cs[0], cs[-1]
        assert cs == list(range(c0, c1 + 1))
        e0 = 7 + 8 * b + c0 - rk
        assert 0 <= e0 and e0 + (c1 - c0) <= 14
        out.append((c0, c1, e0))
    return out


class Prog:
    def __init__(self, depth=DEPTH, debug=False, stop=None):
        self.stop = stop
        self.depth = depth
        self.debug = debug
        self.nc = nc = bass.Bass("TRN2", target_bir_lowering=False)
        self.top = ExitStack()
        self.sy = Sync(nc, self.top)
        ein = lambda n, sh, dt=F32: nc.dram_tensor(n, list(sh), dt, kind="ExternalInput").ap()
        skind = "ExternalOutput" if debug else "Internal"
        scr = lambda n, sh, dt: nc.dram_tensor(n, list(sh), dt, kind=skind).ap()
        self.x = ein("x", [NSEQ, S, D])
        self.p = ein("p", [DEPTH, NSEQ, S, PLE])
        self.norm_mix = ein("norm_mix", [DEPTH, D])
        self.norm_ffn = ein("norm_ffn", [DEPTH, D])
        self.norm_ple = ein("norm_ple", [DEPTH, D])
        self.norm_final = ein("norm_final", [D])
        self.da_w_qkv = ein("da_w_qkv", [2, D, 3 * D])
        self.da_w_o = ein("da_w_o", [2, D, D])
        self.da_lq1 = ein("da_lambda_q1", [2, 64])
        self.da_lk1 = ein("da_lambda_k1", [2, 64])
        self.da_lq2 = ein("da_lambda_q2", [2, 64])
        self.da_lk2 = ein("da_lambda_k2", [2, 64])
        self.da_subln = ein("da_subln", [2, 128])
        self.na_w_qkv = ein("na_w_qkv", [2, D, 3 * D])
        self.na_b_qkv = ein("na_b_qkv", [2, 3 * D])
        self.na_w_o = ein("na_w_o", [2, D, D])
        self.na_tab = ein("na_tab", [2, 16, 64, 15 * 64])
        self.w_gate = ein("ffn_w_gate", [DEPTH, D, FF])
        self.w_up = ein("ffn_w_up", [DEPTH, D, FF])
        self.w_down = ein("ffn_w_down", [DEPTH, FF, D])
        self.w_pproj = ein("ple_w_proj", [DEPTH, PLE, D])
        self.w_pgate = ein("ple_w_gate", [DEPTH, D, D])
        self.c_ident = ein("c_ident", [128, 128])
        self.c_qaug = ein("c_qaug", [4, S], BF16)
        self.c_kaug = ein("c_kaug", [8, 2, 4, S], BF16)
        self.c_cbase = ein("c_cbase", [4, 128, 512])
        self.out = nc.dram_tensor("out", [NSEQ, S, D], F32, kind="ExternalOutput").ap()
        self.hT = scr("hT", [NSEQ, 8, 128, S], F32)
        self.qT = scr("qT", [NSEQ, 8, 128, S], BF16)
        self.kT = scr("kT", [NSEQ, 8, 128, S], BF16)
        self.vS = scr("vS", [NSEQ, S, D], BF16)
        self.oT = scr("oT", [NSEQ, 8, 128, S], BF16)
        self.aT = scr("aT", [NSEQ, FC, 128, S], BF16)
        self.tabb = scr("tabb", [16, 64, 15 * 64], BF16)
        mk = lambda n: [[Buf(f"{n}{s}_{j}") for j in range(NBLK)] for s in range(NSEQ)]
        self.b_hT = mk("hT")
        self.b_qk = mk("qk")
        self.b_v = mk("v")
        self.b_oT = mk("oT")
        self.b_aT = mk("aT")
        self.b_tab = Buf("tabb")

    def sb(self, st, name, shape, dt):
        self._uid = getattr(self, "_uid", 0) + 1
        return st.enter_context(self.nc.sbuf_tensor(f"{name}_{self._uid}", list(shape), dt))

    def build(self):
        nc, sy = self.nc, self.sy
        top = self.top
        self.ident_f = self.sb(top, "ident_f", [128, 128], F32)
        self.ident_b = self.sb(top, "ident_b", [128, 128], BF16)
        self.ones_f = self.sb(top, "ones_f", [128, 128], F32)
        self.gains = self.sb(top, "gains", [128, 13, 8], F32)
        self.b_const = Buf("const")
        self.ps = [top.enter_context(nc.psum_tensor(f"ps{i}", [128, 512], F32)) for i in range(7)]
        self.b_ps = [Buf(f"ps{i}") for i in range(7)]
        cb = self.b_const
        sy.dma("sp", self.ident_f[:], self.c_ident, cb, writes=[cb])
        for i, vec in enumerate([self.norm_mix, self.norm_ffn, self.norm_ple]):
            for l in range(DEPTH):
                sy.dma("sp", self.gains[:, i * 4 + l, :], vec[l].rearrange("(c p) -> p c", p=128), cb, writes=[cb], slow=True)
        sy.dma("sp", self.gains[:, 12, :], self.norm_final.rearrange("(c p) -> p c", p=128), cb, writes=[cb], slow=True)
        sy.op("dve", lambda: nc.vector.tensor_copy(self.ident_b[:], self.ident_f[:]), reads=[cb], writes=[cb])
        sy.op("dve", lambda: nc.vector.memset(self.ones_f[:], 1.0), writes=[cb])
        sy.barrier()

        def run_layers():
            for l in range(self.depth):
                j = l // 2
                if l % 2 == 0:
                    self.phase_qkv(l, self.da_w_qkv[j], None, from_x=(l == 0))
                    sy.barrier()
                    if self.stop == f"qkv{l}":
                        return False
                    self.phase_da(l, j)
                    sy.barrier()
                    wo = self.da_w_o[j]
                else:
                    self.phase_qkv(l, self.na_w_qkv[j], self.na_b_qkv[j], from_x=False)
                    sy.barrier()
                    if self.stop == f"qkv{l}":
                        return False
                    self.phase_na(l, j)
                    sy.barrier()
                    wo = self.na_w_o[j]
                if self.stop == f"attn{l}":
                    return False
                self.phase_c(l, wo)
                sy.barrier()
                if self.stop == f"c{l}":
                    return False
                self.phase_d(l)
                sy.barrier()
                if self.stop == f"d{l}":
                    return False
            return True

        if self.stop == "natest":
            self.phase_qkv(1, self.na_w_qkv[0], self.na_b_qkv[0], from_x=True)
            sy.barrier()
            self.phase_na(1, 0)
            sy.barrier()
            top.close()
            return nc
        if not run_layers():
            top.close()
            return nc
        self.phase_final()
        sy.barrier()
        top.close()
        return nc

    def gain(self, kind, l):
        return self.gains[:, kind * 4 + l, :]

    def load_w(self, st, name, Wd, KC, N, gain, mul, stg, stgb, ctr, PW=1024):
        nc, sy = self.nc, self.sy
        Wt = self.sb(st, name, [128, KC, N], BF16)
        bufs = [Buf(f"{name}{c}") for c in range(KC)]
        for c in range(KC):
            eng = "dve" if c % 2 == 0 else "pool"
            for n0 in range(0, N, PW):
                w = min(PW, N - n0)
                i = ctr[0] % len(stg)
                ctr[0] += 1
                sy.dma("sp", stg[i][:, :w], Wd[c * 128:(c + 1) * 128, n0:n0 + w], stgb[i], writes=[stgb[i]])
                e = nc.vector if eng == "dve" else nc.gpsimd
                sc1 = gain[:, c:c + 1] if gain is not None else 1.0
                sy.op(eng, lambda e=e, i=i, w=w, c=c, n0=n0, sc1=sc1: e.tensor_scalar(
                    out=Wt[:, c, n0:n0 + w], in0=stg[i][:, :w], scalar1=sc1, scalar2=float(mul),
                    op0=ALU.mult, op1=ALU.mult),
                    reads=[stgb[i], self.b_const], writes=[bufs[c]])
        return Wt, bufs

    def rms_block(self, hin, hb, sq, sqb, rstd, rstdb, psi, eps=1e-6):
        nc, sy = self.nc, self.sy
        for c in range(8):
            i = c % 2
            sy.op("act", lambda c=c, i=i: nc.scalar.activation(out=sq[i][:], in_=hin[:, c, :], func=AF.Square),
                  reads=[hb], writes=[sqb[i]])
            sy.op("pe", lambda c=c, i=i: nc.tensor.matmul(self.ps[psi][:], lhsT=self.ones_f[:], rhs=sq[i][:],
                                                          start=(c == 0), stop=(c == 7)),
                  reads=[sqb[i], self.b_const], writes=[self.b_ps[psi]])
        sy.op("act", lambda: nc.scalar.activation(out=rstd[:], in_=self.ps[psi][:], func=AF.Sqrt,
                                                  scale=1.0 / D, bias=self.epsc[:, 0:1]),
              reads=[self.b_ps[psi], self.b_const], writes=[rstdb])
        sy.op("dve", lambda: nc.vector.reciprocal(rstd[:], rstd[:]), reads=[rstdb], writes=[rstdb])

    def phase_qkv(self, l, Wd, bias_d, from_x):
        nc, sy = self.nc, self.sy
        with ExitStack() as st:
            stg = [self.sb(st, f"a_stg{i}", [128, 1024], F32) for i in range(3)]
            stgb = [Buf(f"a_stg{i}") for i in range(3)]
            ctr = [0]
            g = self.gain(0, l)
            W, wb = [None] * 3, [None] * 3
            for t in range(3):
                W[t], wb[t] = self.load_w(st, f"a_w{t}", Wd[:, t * D:(t + 1) * D], 8, D, g,
                                          0.125 if t == 0 else 1.0, stg, stgb, ctr)
            self.epsc = self.sb(st, "a_eps", [128, 1], F32)
            sy.op("dve", lambda: nc.vector.memset(self.epsc[:], 1e-6), writes=[self.b_const])
            if bias_d is not None:
                bqk = self.sb(st, "a_bqk", [128, 16], F32)
                bv = self.sb(st, "a_bv", [128, D], F32)
                bb = Buf("a_bias")
                sy.dma("sp", bqk[:], bias_d[0:2 * D].rearrange("(n p) -> p n", p=128), bb, writes=[bb], slow=True)
                sy.dma("sp", bv[:], bias_d[2 * D:3 * D].partition_broadcast(128), bb, writes=[bb])
                sy.op("dve", lambda: nc.vector.tensor_scalar(out=bqk[:, 0:8], in0=bqk[:, 0:8], scalar1=0.125,
                                                            scalar2=None, op0=ALU.mult), reads=[bb], writes=[bb])
            hin = [self.sb(st, f"a_hin{i}", [128, 8, TB], F32) for i in range(2)]
            hb = [Buf(f"a_hin{i}") for i in range(2)]
            if from_x:
                xin = [self.sb(st, f"a_xin{i}", [128, 4, D], F32) for i in range(1)]
                xb = [Buf(f"a_xin{i}") for i in range(1)]
            sq = [self.sb(st, f"a_sq{i}", [128, TB], F32) for i in range(2)]
            sqb = [Buf(f"a_sq{i}") for i in range(2)]
            rstd = self.sb(st, "a_rstd", [128, TB], F32)
            rstdb = Buf("a_rstd")
            xn = [self.sb(st, f"a_xn{i}", [128, 8, TB], BF16) for i in range(2)]
            xnb = [Buf(f"a_xn{i}") for i in range(2)]
            qks = [self.sb(st, f"a_qks{i}", [128, 16, TB], BF16) for i in range(2)]
            qksb = [Buf(f"a_qks{i}") for i in range(2)]
            vst = [self.sb(st, f"a_vst{i}", [128, 4, D], BF16) for i in range(2)]
            vstb = [Buf(f"a_vst{i}") for i in range(2)]
            rot = [0]
            it = 0
            for s in range(NSEQ):
                for jb in range(NBLK):
                    i2 = it % 2
                    it += 1
                    tsl = slice(jb * TB, (jb + 1) * TB)
                    h, hbb = hin[i2], hb[i2]
                    if from_x:
                        sy.dma("sp", xin[0][:], self.x[s, tsl, :].rearrange("(a p) d -> p a d", p=128), xb[0],
                               writes=[xb[0]])
                        for c in range(8):
                            pi = 1 + (rot[0] % 6)
                            rot[0] += 1
                            for a in range(4):
                                sy.op("pe", lambda a=a, c=c, pi=pi: nc.tensor.transpose(
                                    self.ps[pi][:, a * 128:(a + 1) * 128], xin[0][:, a, c * 128:(c + 1) * 128],
                                    self.ident_f[:]), signal=(a == 3), reads=[xb[0], self.b_const], writes=[self.b_ps[pi]])
                            eng = "act" if c % 2 == 0 else "dve"
                            if eng == "act":
                                sy.op("act", lambda c=c, pi=pi: nc.scalar.copy(h[:, c, :], self.ps[pi][:]),
                                      reads=[self.b_ps[pi]], writes=[hbb])
                            else:
                                sy.op("dve", lambda c=c, pi=pi: nc.vector.tensor_copy(h[:, c, :], self.ps[pi][:]),
                                      reads=[self.b_ps[pi]], writes=[hbb])
                        sy.dma("pool", self.hT[s, :, :, tsl].rearrange("c p t -> p c t"), h[:], hbb,
                               reads=[hbb], writes=[self.b_hT[s][jb]])
                    else:
                        sy.dma("sp", h[:], self.hT[s, :, :, tsl].rearrange("c p t -> p c t"), hbb,
                               reads=[self.b_hT[s][jb]], writes=[hbb])
                    self.rms_block(h, hbb, sq, sqb, rstd, rstdb, 0)
                    for c in range(8):
                        eng = "dve" if c % 2 == 0 else "pool"
                        e = nc.vector if eng == "dve" else nc.gpsimd
                        sy.op(eng, lambda e=e, c=c: e.tensor_tensor(out=xn[i2][:, c, :], in0=h[:, c, :], in1=rstd[:],
                                                                    op=ALU.mult),
                              reads=[hbb, rstdb], writes=[xnb[i2]])
                    for n in range(16):
                        t, nn = divmod(n, 8)
                        pi = 1 + (rot[0] % 6)
                        rot[0] += 1
                        for c in range(8):
                            sy.op("pe", lambda c=c, t=t, nn=nn, pi=pi: nc.tensor.matmul(
                                self.ps[pi][:], lhsT=W[t][:, c, nn * 128:(nn + 1) * 128], rhs=xn[i2][:, c, :],
                                start=(c == 0), stop=(c == 7)), signal=(c == 7),
                                reads=[wb[t][c], xnb[i2]], writes=[self.b_ps[pi]])
                        if n % 2 == 0:
                            if bias_d is None:
                                sy.op("act", lambda n=n, pi=pi: nc.scalar.copy(qks[i2][:, n, :], self.ps[pi][:]),
                                      reads=[self.b_ps[pi]], writes=[qksb[i2]])
                            else:
                                sy.op("act", lambda n=n, pi=pi: nc.scalar.activation(
                                    out=qks[i2][:, n, :], in_=self.ps[pi][:], func=AF.Identity, bias=bqk[:, n:n + 1]),
                                    reads=[self.b_ps[pi], bb], writes=[qksb[i2]])
                        else:
                            if bias_d is None:
                                sy.op("dve", lambda n=n, pi=pi: nc.vector.tensor_copy(qks[i2][:, n, :], self.ps[pi][:]),
                                      reads=[self.b_ps[pi]], writes=[qksb[i2]])
                            else:
                                sy.op("dve", lambda n=n, pi=pi: nc.vector.tensor_scalar(
                                    out=qks[i2][:, n, :], in0=self.ps[pi][:], scalar1=bqk[:, n:n + 1], scalar2=None,
                                    op0=ALU.add), reads=[self.b_ps[pi], bb], writes=[qksb[i2]])
                    sy.dma("pool", self.qT[s, :, :, tsl].rearrange("c p t -> p c t"), qks[i2][:, 0:8, :], qksb[i2],
                           reads=[qksb[i2]], writes=[self.b_qk[s][jb]])
                    sy.dma("pool", self.kT[s, :, :, tsl].rearrange("c p t -> p c t"), qks[i2][:, 8:16, :], qksb[i2],
                           reads=[qksb[i2]], writes=[self.b_qk[s][jb]])
                    for a in range(4):
                        for nb in range(2):
                            pi = 1 + (rot[0] % 6)
                            rot[0] += 1
                            for c in range(8):
                                sy.op("pe", lambda c=c, a=a, nb=nb, pi=pi: nc.tensor.matmul(
                                    self.ps[pi][:], lhsT=xn[i2][:, c, a * 128:(a + 1) * 128],
                                    rhs=W[2][:, c, nb * 512:(nb + 1) * 512], start=(c == 0), stop=(c == 7)), signal=(c == 7),
                                    reads=[wb[2][c], xnb[i2]], writes=[self.b_ps[pi]])
                            if bias_d is None:
                                if nb == 0:
                                    sy.op("act", lambda a=a, nb=nb, pi=pi: nc.scalar.copy(
                                        vst[i2][:, a, nb * 512:(nb + 1) * 512], self.ps[pi][:]),
                                        reads=[self.b_ps[pi]], writes=[vstb[i2]])
                                else:
                                    sy.op("dve", lambda a=a, nb=nb, pi=pi: nc.vector.tensor_copy(
                                        vst[i2][:, a, nb * 512:(nb + 1) * 512], self.ps[pi][:]),
                                        reads=[self.b_ps[pi]], writes=[vstb[i2]])
                            else:
                                sy.op("dve", lambda a=a, nb=nb, pi=pi: nc.vector.tensor_tensor(
                                    out=vst[i2][:, a, nb * 512:(nb + 1) * 512], in0=self.ps[pi][:],
                                    in1=bv[:, nb * 512:(nb + 1) * 512], op=ALU.add),
                                    reads=[self.b_ps[pi], bb], writes=[vstb[i2]])
                    sy.dma("pool", self.vS[s, tsl, :].rearrange("(a p) n -> p a n", p=128), vst[i2][:], vstb[i2],
                           reads=[vstb[i2]], writes=[self.b_v[s][jb]])

    def phase_da(self, l, j):
        nc, sy = self.nc, self.sy
        lam_init = 0.8 - 0.6 * math.exp(-0.3 * l)
        with ExitStack() as st:
            cst = Buf("b_cst")
            cbase = self.sb(st, "b_cbase", [128, 4, 512], F32)
            sy.dma("sp", cbase[:], self.c_cbase.rearrange("a p f -> p a f"), cst, writes=[cst])
            lt = self.sb(st, "b_lt", [128, 4, 64], F32)
            for i, v in enumerate([self.da_lq1, self.da_lk1, self.da_lq2, self.da_lk2]):
                sy.dma("sp", lt[:, i, :], v[j].partition_broadcast(128), cst, writes=[cst])
            gsub = self.sb(st, "b_gsub", [128, 128], F32)
            sy.dma("sp", gsub[:], self.da_subln[j].partition_broadcast(128), cst, writes=[cst])
            sm = self.sb(st, "b_sm", [128, 8], F32)
            prod = self.sb(st, "b_prod", [128, 2, 64], F32)
            sy.op("dve", lambda: nc.vector.tensor_tensor(out=prod[:, 0, :], in0=lt[:, 0, :], in1=lt[:, 1, :], op=ALU.mult),
                  reads=[cst], writes=[cst])
            sy.op("dve", lambda: nc.vector.tensor_tensor(out=prod[:, 1, :], in0=lt[:, 2, :], in1=lt[:, 3, :], op=ALU.mult),
                  reads=[cst], writes=[cst])
            sy.op("dve", lambda: nc.vector.reduce_sum(out=sm[:, 0:2], in_=prod[:], axis=AX.X), reads=[cst], writes=[cst])
            sy.op("act", lambda: nc.scalar.activation(out=sm[:, 2:4], in_=sm[:, 0:2], func=AF.Exp), reads=[cst], writes=[cst])
            sy.op("dve", lambda: nc.vector.tensor_tensor(out=sm[:, 4:5], in0=sm[:, 2:3], in1=sm[:, 3:4], op=ALU.subtract),
                  reads=[cst], writes=[cst])
            sy.op("dve", lambda: nc.vector.tensor_scalar(out=sm[:, 4:5], in0=sm[:, 4:5], scalar1=float(lam_init),
                                                        scalar2=None, op0=ALU.add), reads=[cst], writes=[cst])
            sy.op("dve", lambda: nc.vector.tensor_scalar(out=sm[:, 5:6], in0=sm[:, 4:5], scalar1=-1.0, scalar2=None,
                                                        op0=ALU.mult), reads=[cst], writes=[cst])
            sy.op("dve", lambda: nc.vector.memset(sm[:, 6:7], 1e-5), reads=[cst], writes=[cst])
            sy.op("dve", lambda: nc.vector.tensor_scalar(out=gsub[:], in0=gsub[:], scalar1=float(1.0 - lam_init),
                                                        scalar2=None, op0=ALU.mult), reads=[cst], writes=[cst])
            nlam = sm[:, 5:6]
            NSET = 2
            Q = [[self.sb(st, f"b_q{i}{m}", [128, S], BF16) for m in range(2)] for i in range(NSET)]
            K = [[[self.sb(st, f"b_k{i}{m}{v}", [128, S], BF16) for v in range(2)] for m in range(2)] for i in range(NSET)]
            V = [self.sb(st, f"b_v{i}", [128, 32, 129], BF16) for i in range(NSET)]
            setb = [Buf(f"b_set{i}") for i in range(NSET)]
            for i in range(NSET):
                for m in range(2):
                    for t in [Q[i][m]] + K[i][m]:
                        sy.op("pool", lambda t=t: nc.gpsimd.memset(t[:], 0.0), writes=[setb[i]])
                sy.op("pool", lambda i=i: nc.gpsimd.memset(V[i][:, :, 128:129], 1.0), writes=[setb[i]])
                sy.dma("sp", Q[i][0][64:68, :], self.c_qaug, setb[i], writes=[setb[i]])
                sy.dma("sp", Q[i][1][60:64, :], self.c_qaug, setb[i], writes=[setb[i]])
            PT = [[self.sb(st, f"b_pt{m}{i}", [128, 512], BF16) for i in range(2)] for m in range(2)]
            PTb = [[Buf(f"b_pt{m}{i}") for i in range(2)] for m in range(2)]
            dtmp = [self.sb(st, f"b_dtmp{m}", [128, 512], F32) for m in range(2)]
            dtmpb = [Buf(f"b_dtmp{m}") for m in range(2)]
            oev = self.sb(st, "b_oev", [128, 3, 387], F32)
            oevb = Buf("b_oev")
            wk = self.sb(st, "b_wk", [128, 32], F32)
            of32 = self.sb(st, "b_of32", [128, 4, 128], F32)
            junk = self.sb(st, "b_junk", [128, 128], F32)
            onb = self.sb(st, "b_onb", [128, 4, 128], BF16)
            cmb = Buf("b_cmb")
            pst = st.enter_context(nc.psum_tensor(f"b_pst{l}", [128, 1024], BF16))
            pstb = Buf("b_pst")
            ost = [self.sb(st, f"b_ost{i}", [128, 512], BF16) for i in range(2)]
            ostb = [Buf(f"b_ost{i}") for i in range(2)]
            def oacc(m, qs):
                a = m * 4 + qs
                return self.ps[4 + a // 3][:, (a % 3) * 129:(a % 3) * 129 + 129], 4 + a // 3, (a % 3 == 0)

            hs = 0
            ocnt = 0
            for s in range(NSEQ):
                for h in range(8):
                    si = hs % NSET
                    hs += 1
                    sb_ = setb[si]
                    dsrc = [b for b in self.b_qk[s]] + [b for b in self.b_v[s]]
                    sy.dma("sp", Q[si][0][0:64, :], self.qT[s, h, 0:64, :], sb_, reads=dsrc, writes=[sb_])
                    sy.dma("sp", Q[si][1][64:128, :], self.qT[s, h, 64:128, :], sb_, reads=dsrc, writes=[sb_])
                    for v in range(2):
                        sy.dma("sp", K[si][0][v][0:64, :], self.kT[s, h, 0:64, :], sb_, reads=dsrc, writes=[sb_])
                        sy.dma("sp", K[si][1][v][64:128, :], self.kT[s, h, 64:128, :], sb_, reads=dsrc, writes=[sb_])
                        sy.dma("sp", K[si][0][v][64:68, :], self.c_kaug[h, v], sb_, writes=[sb_])
                        sy.dma("sp", K[si][1][v][60:64, :], self.c_kaug[h, v], sb_, writes=[sb_])
                    sy.dma("sp", V[si][:, :, 0:128],
                           self.vS[s, :, h * 128:(h + 1) * 128].rearrange("(t p) d -> p t d", p=128), sb_,
                           reads=dsrc, writes=[sb_])
                    slope = 2.0 ** (-(h + 1))
                    units = [(qb, kt) for qb in range(8) for kt in range(32)]

                    def qk(u):
                        qb, kt = units[u]
                        if kt * 128 + 127 < qb * 512:
                            var = 0
                        elif kt * 128 > qb * 512 + 511:
                            var = 1
                        else:
                            var = 0
                        for m in range(2):
                            pi = m * 2 + (u % 2)
                            sy.op("pe", lambda m=m, pi=pi, var=var, qb=qb, kt=kt: nc.tensor.matmul(
                                self.ps[pi][:], lhsT=K[si][m][var][:, kt * 128:(kt + 1) * 128],
                                rhs=Q[si][m][:, qb * 512:(qb + 1) * 512], start=True, stop=True),
                                reads=[sb_], writes=[self.b_ps[pi]])

                    def expav(u):
                        qb, kt = units[u]
                        diag = (4 * qb <= kt <= 4 * qb + 3)
                        for m in range(2):
                            pi = m * 2 + (u % 2)
                            pt, ptb = PT[m][u % 2], PTb[m][u % 2]
                            if diag:
                                dd = kt - 4 * qb
                                sy.op("dve", lambda m=m, pi=pi, dd=dd: nc.vector.scalar_tensor_tensor(
                                    out=dtmp[m][:], in0=cbase[:, dd, :], scalar=float(-2.0 * slope), in1=self.ps[pi][:],
                                    op0=ALU.mult, op1=ALU.add), reads=[cst, self.b_ps[pi]], writes=[dtmpb[m]])
                                sy.op("act", lambda m=m, pt=pt: nc.scalar.activation(out=pt[:], in_=dtmp[m][:], func=AF.Exp),
                                      reads=[dtmpb[m]], writes=[ptb])
                            else:
                                sy.op("act", lambda pi=pi, pt=pt: nc.scalar.activation(out=pt[:], in_=self.ps[pi][:], func=AF.Exp),
                                      reads=[self.b_ps[pi]], writes=[ptb])
                        for m in range(2):
                            pt, ptb = PT[m][u % 2], PTb[m][u % 2]
                            for qs in range(4):
                                oap, bank, first = oacc(m, qs)
                                sy.op("pe", lambda pt=pt, qs=qs, oap=oap, kt=kt, first=first: nc.tensor.matmul(
                                    oap, lhsT=pt[:, qs * 128:(qs + 1) * 128], rhs=V[si][:, kt, :],
                                    start=(kt == 0 and first), stop=(kt == 31), skip_group_check=True),
                                    signal=(qs == 3), reads=[ptb, sb_], writes=[self.b_ps[bank]])

                    def finish(qb):
                        nonlocal ocnt
                        for bk in range(3):
                            e = "dve" if bk != 1 else "act"
                            if e == "dve":
                                sy.op("dve", lambda bk=bk: nc.vector.tensor_copy(oev[:, bk, :], self.ps[4 + bk][:, 0:387]),
                                      reads=[self.b_ps[4 + bk]], writes=[oevb])
                            else:
                                sy.op("act", lambda bk=bk: nc.scalar.copy(oev[:, bk, :], self.ps[4 + bk][:, 0:387]),
                                      reads=[self.b_ps[4 + bk]], writes=[oevb])
                        ov = oev[:].rearrange("p a (b n) -> p (a b) n", n=129)
                        sy.op("dve", lambda: nc.vector.reciprocal(wk[:, 0:8], ov[:, 0:8, 128]), reads=[oevb], writes=[cmb])
                        sy.op("dve", lambda: nc.vector.tensor_scalar(out=wk[:, 8:12], in0=wk[:, 4:8], scalar1=nlam,
                                                                    scalar2=None, op0=ALU.mult), reads=[cmb, cst], writes=[cmb])
                        for qs in range(4):
                            sy.op("dve", lambda qs=qs: nc.vector.tensor_scalar(
                                out=of32[:, qs, :], in0=ov[:, qs, 0:128], scalar1=wk[:, qs:qs + 1], scalar2=None,
                                op0=ALU.mult), reads=[oevb, cmb], writes=[cmb])
                            sy.op("dve", lambda qs=qs: nc.vector.scalar_tensor_tensor(
                                out=of32[:, qs, :], in0=ov[:, 4 + qs, 0:128], scalar=wk[:, 8 + qs:9 + qs],
                                in1=of32[:, qs, :], op0=ALU.mult, op1=ALU.add), reads=[oevb, cmb], writes=[cmb])
                            sy.op("act", lambda qs=qs: nc.scalar.activation(
                                out=junk[:], in_=of32[:, qs, :], func=AF.Square, accum_out=wk[:, 12 + qs:13 + qs]),
                                reads=[cmb], writes=[cmb])
                        sy.op("act", lambda: nc.scalar.activation(out=wk[:, 16:20], in_=wk[:, 12:16], func=AF.Sqrt,
                                                                  scale=1.0 / 128.0, bias=sm[:, 6:7]),
                              reads=[cmb, cst], writes=[cmb])
                        sy.op("dve", lambda: nc.vector.reciprocal(wk[:, 16:20], wk[:, 16:20]), reads=[cmb], writes=[cmb])
                        for qs in range(4):
                            sy.op("dve", lambda qs=qs: nc.vector.scalar_tensor_tensor(
                                out=onb[:, qs, :], in0=of32[:, qs, :], scalar=wk[:, 16 + qs:17 + qs], in1=gsub[:],
                                op0=ALU.mult, op1=ALU.mult), reads=[cmb, cst], writes=[cmb])
                        for qs in range(4):
                            sy.op("pe", lambda qs=qs: nc.tensor.transpose(pst[:, qs * 128:(qs + 1) * 128], onb[:, qs, :],
                                                                          self.ident_b[:]),
                                  signal=(qs == 3), reads=[cmb, self.b_const], writes=[pstb])
                        oi = ocnt % 2
                        ocnt += 1
                        sy.op("dve", lambda oi=oi: nc.vector.tensor_copy(ost[oi][:], pst[:, 0:512]), reads=[pstb],
                              writes=[ostb[oi]])
                        sy.dma("pool", self.oT[s, h, :, qb * 512:(qb + 1) * 512], ost[oi][:], ostb[oi],
                               reads=[ostb[oi]], writes=[self.b_oT[s][qb]])

                    qk(0)
                    for u in range(len(units)):
                        if u + 1 < len(units):
                            qk(u + 1)
                        expav(u)
                        if units[u][1] == 31:
                            finish(units[u][0])

    def phase_na(self, l, j):
        nc, sy = self.nc, self.sy
        with ExitStack() as st:
            tst = [self.sb(st, f"n_tst{i}", [128, 480], F32) for i in range(2)]
            tsb = [self.sb(st, f"n_tsb{i}", [128, 480], BF16) for i in range(2)]
            tstb = [Buf(f"n_tst{i}") for i in range(2)]
            tsbb = [Buf(f"n_tsb{i}") for i in range(2)]
            tabf = self.na_tab[j].rearrange("h w (a f) -> (h w a) f", f=480)
            tabo = self.tabb.rearrange("h w (a f) -> (h w a) f", f=480)
            for i in range(16):
                k = i % 2
                sy.dma("sp", tst[k][:], tabf[i * 128:(i + 1) * 128, :], tstb[k], writes=[tstb[k]])
                sy.op("dve", lambda k=k: nc.vector.tensor_copy(tsb[k][:], tst[k][:]), reads=[tstb[k]], writes=[tsbb[k]])
                sy.dma("pool", tabo[i * 128:(i + 1) * 128, :], tsb[k][:], tsbb[k], reads=[tsbb[k]], writes=[self.b_tab])
            tab4 = self.tabb.rearrange("h w (e q) -> h w e q", q=64)
            slot0 = {0: 0, 1: 6, 2: 14}
            BT = [self.sb(st, f"n_bt{hh}", [128, 20, 512], BF16) for hh in range(2)]
            BTb = [Buf(f"n_bt{hh}") for hh in range(2)]
            for hh in range(2):
                sy.op("pool", lambda hh=hh: nc.gpsimd.memset(BT[hh][:], MASKV), writes=[BTb[hh]])
            NSET = 2
            Qc = [self.sb(st, f"n_q{i}", [128, S], BF16) for i in range(NSET)]
            Kc = [[self.sb(st, f"n_k{i}{hh}", [128, S], BF16) for hh in range(2)] for i in range(NSET)]
            Vc = [self.sb(st, f"n_v{i}", [128, 32, 2, 65], BF16) for i in range(NSET)]
            setb = [Buf(f"n_set{i}") for i in range(NSET)]
            for i in range(NSET):
                for hh in range(2):
                    sy.op("pool", lambda i=i, hh=hh: nc.gpsimd.memset(Kc[i][hh][:], 0.0), writes=[setb[i]])
                sy.op("pool", lambda i=i: nc.gpsimd.memset(Vc[i][:, :, :, 64:65], 1.0), writes=[setb[i]])
            PT = [self.sb(st, f"n_pt{i}", [128, 512], BF16) for i in range(3)]
            PTb = [Buf(f"n_pt{i}") for i in range(3)]
            oev = self.sb(st, "n_oev", [128, 4, 65], F32)
            oevb = Buf("n_oev")
            rl = self.sb(st, "n_rl", [128, 4], F32)
            opair = self.sb(st, "n_opair", [128, 4, 128], BF16)
            opb = Buf("n_opair")
            pst = st.enter_context(nc.psum_tensor(f"n_pst{l}", [128, 1024], BF16))
            pstb = Buf("n_pst")
            ost = [self.sb(st, f"n_ost{i}", [128, 512], BF16) for i in range(2)]
            ostb = [Buf(f"n_ost{i}") for i in range(2)]
            cs = 0
            ocnt = 0
            for c in range(8):
                for hh in range(2):
                    h = 2 * c + hh
                    for b in (0, 1, 7):
                        bt = _na_btype(b)
                        for ti, t in enumerate(_na_tiles(b)):
                            runs = _na_valid_runs(b, t)
                            for a in range(2):
                                if runs[a] is None:
                                    continue
                                c0, c1, e0 = runs[a]
                                n = c1 - c0 + 1
                                sy.dma("sp", BT[hh][a * 64:(a + 1) * 64, slot0[bt] + ti, c0 * 64:(c1 + 1) * 64]
                                       .rearrange("p (e q) -> p e q", q=64),
                                       tab4[h, :, e0:e0 + n, :], BTb[hh], reads=[self.b_tab], writes=[BTb[hh]])
                for s in range(NSEQ):
                    si = cs % NSET
                    cs += 1
                    sb_ = setb[si]
                    dsrc = [b for b in self.b_qk[s]] + [b for b in self.b_v[s]]
                    sy.dma("sp", Qc[si][:], self.qT[s, c], sb_, reads=dsrc, writes=[sb_])
                    sy.dma("sp", Kc[si][0][0:64, :], self.kT[s, c, 0:64, :], sb_, reads=dsrc, writes=[sb_])
                    sy.dma("sp", Kc[si][1][64:128, :], self.kT[s, c, 64:128, :], sb_, reads=dsrc, writes=[sb_])
                    for hh in range(2):
                        sy.dma("sp", Vc[si][:, :, hh, 0:64],
                               self.vS[s, :, c * 128 + hh * 64:c * 128 + (hh + 1) * 64].rearrange("(t p) d -> p t d", p=128),
                               sb_, reads=dsrc, writes=[sb_])
                    units = []
                    for b in range(8):
                        tiles = _na_tiles(b)
                        for hh in range(2):
                            for ti, t in enumerate(tiles):
                                units.append((b, hh, ti, t, len(tiles)))

                    def qk(u):
                        b, hh, ti, t, nt = units[u]
                        bt = _na_btype(b)
                        pi = u % 3
                        sy.op("pe", lambda: nc.tensor.matmul(
                            self.ps[pi][:], lhsT=Kc[si][hh][:, t * 128:(t + 1) * 128],
                            rhs=Qc[si][:, b * 512:(b + 1) * 512], start=True, stop=False), signal=False,
                            reads=[sb_], writes=[self.b_ps[pi]])
                        sy.op("pe", lambda: nc.tensor.matmul(
                            self.ps[pi][:], lhsT=self.ident_b[:], rhs=BT[hh][:, slot0[bt] + ti, :],
                            start=False, stop=True), reads=[BTb[hh], self.b_const], writes=[self.b_ps[pi]])

                    def expav(u):
                        b, hh, ti, t, nt = units[u]
                        pi = u % 3
                        ob = 4 + hh
                        sy.op("act", lambda: nc.scalar.activation(out=PT[pi][:], in_=self.ps[pi][:], func=AF.Exp),
                              reads=[self.b_ps[pi]], writes=[PTb[pi]])
                        for qs in range(4):
                            sy.op("pe", lambda qs=qs: nc.tensor.matmul(
                                self.ps[ob][:, qs * 65:(qs + 1) * 65], lhsT=PT[pi][:, qs * 128:(qs + 1) * 128],
                                rhs=Vc[si][:, t, hh, :], start=(ti == 0 and qs == 0), stop=(ti == nt - 1),
                                skip_group_check=True), signal=(qs == 3), reads=[PTb[pi], sb_], writes=[self.b_ps[ob]])

                    def fin_head(b, hh):
                        ob = 4 + hh
                        sy.op("dve", lambda: nc.vector.tensor_copy(
                            oev[:].rearrange("p a n -> p (a n)"), self.ps[ob][:, 0:260]),
                            reads=[self.b_ps[ob]], writes=[oevb])
                        sy.op("dve", lambda: nc.vector.reciprocal(rl[:], oev[:, :, 64]), reads=[oevb], writes=[oevb])
                        for qs in range(4):
                            sy.op("dve", lambda qs=qs: nc.vector.tensor_scalar(
                                out=opair[:, qs, hh * 64:(hh + 1) * 64], in0=oev[:, qs, 0:64], scalar1=rl[:, qs:qs + 1],
                                scalar2=None, op0=ALU.mult), reads=[oevb], writes=[opb])

                    def fin_block(b):
                        nonlocal ocnt
                        for qs in range(4):
                            sy.op("pe", lambda qs=qs: nc.tensor.transpose(pst[:, qs * 128:(qs + 1) * 128], opair[:, qs, :],
                                                                          self.ident_b[:]),
                                  signal=(qs == 3), reads=[opb, self.b_const], writes=[pstb])
                        oi = ocnt % 2
                        ocnt += 1
                        sy.op("act", lambda: nc.scalar.copy(ost[oi][:], pst[:, 0:512]), reads=[pstb], writes=[ostb[oi]])
                        sy.dma("pool", self.oT[s, c, :, b * 512:(b + 1) * 512], ost[oi][:], ostb[oi],
                               reads=[ostb[oi]], writes=[self.b_oT[s][b]])

                    qk(0)
                    for u in range(len(units)):
                        if u + 1 < len(units):
                            qk(u + 1)
                        expav(u)
                        b, hh, ti, t, nt = units[u]
                        if ti == nt - 1:
                            fin_head(b, hh)
                            if hh == 1:
                                fin_block(b)

    def phase_c(self, l, Wo_d):
        nc, sy = self.nc, self.sy
        with ExitStack() as st:
            stg = [self.sb(st, f"c_stg{i}", [128, 1408], F32) for i in range(3)]
            stgb = [Buf(f"c_stg{i}") for i in range(3)]
            ctr = [0]
            Wo, wob = self.load_w(st, "c_wo", Wo_d, 8, D, None, 1.0, stg, stgb, ctr)
            g = self.gain(1, l)
            Wg, wgb = self.load_w(st, "c_wg", self.w_gate[l], 8, FF, g, 1.0, stg, stgb, ctr, PW=1408)
            Wu, wub = self.load_w(st, "c_wu", self.w_up[l], 8, FF, g, 1.0, stg, stgb, ctr, PW=1408)
            self.epsc = self.sb(st, "c_eps", [128, 1], F32)
            sy.op("dve", lambda: nc.vector.memset(self.epsc[:], 1e-6), writes=[self.b_const])
            oin = [self.sb(st, f"c_oin{i}", [128, 8, TB], BF16) for i in range(2)]
            oib = [Buf(f"c_oin{i}") for i in range(2)]
            hin = [self.sb(st, f"c_hin{i}", [128, 8, TB], F32) for i in range(2)]
            hb = [Buf(f"c_hin{i}") for i in range(2)]
            sq = [self.sb(st, f"c_sq{i}", [128, TB], F32) for i in range(2)]
            sqb = [Buf(f"c_sq{i}") for i in range(2)]
            rstd = self.sb(st, "c_rstd", [128, TB], F32)
            rstdb = Buf("c_rstd")
            xn = self.sb(st, "c_xn", [128, 8, TB], BF16)
            xnb = Buf("c_xn")
            sg = [self.sb(st, f"c_sg{i}", [128, TB], F32) for i in range(2)]
            sgb = [Buf(f"c_sg{i}") for i in range(2)]
            ast = [self.sb(st, f"c_ast{i}", [128, 2, TB], BF16) for i in range(3)]
            astb = [Buf(f"c_ast{i}") for i in range(3)]
            rot = [0]
            it = 0
            ai = 0
            for s in range(NSEQ):
                for jb in range(NBLK):
                    i2 = it % 2
                    it += 1
                    tsl = slice(jb * TB, (jb + 1) * TB)
                    h, hbb = hin[i2], hb[i2]
                    sy.dma("sp", oin[i2][:], self.oT[s, :, :, tsl].rearrange("c p t -> p c t"), oib[i2],
                           reads=[self.b_oT[s][jb]], writes=[oib[i2]])
                    sy.dma("sp", h[:], self.hT[s, :, :, tsl].rearrange("c p t -> p c t"), hbb,
                           reads=[self.b_hT[s][jb]], writes=[hbb])
                    for n in range(8):
                        pi = 1 + (rot[0] % 6)
                        rot[0] += 1
                        for c in range(8):
                            sy.op("pe", lambda c=c, n=n, pi=pi: nc.tensor.matmul(
                                self.ps[pi][:], lhsT=Wo[:, c, n * 128:(n + 1) * 128], rhs=oin[i2][:, c, :],
                                start=(c == 0), stop=(c == 7)), signal=(c == 7), reads=[wob[c], oib[i2]], writes=[self.b_ps[pi]])
                        sy.op("dve", lambda n=n, pi=pi: nc.vector.tensor_tensor(out=h[:, n, :], in0=h[:, n, :],
                                                                                in1=self.ps[pi][:], op=ALU.add),
                              reads=[self.b_ps[pi]], writes=[hbb])
                    sy.dma("pool", self.hT[s, :, :, tsl].rearrange("c p t -> p c t"), h[:], hbb,
                           reads=[hbb], writes=[self.b_hT[s][jb]])
                    self.rms_block(h, hbb, sq, sqb, rstd, rstdb, 0)
                    for c in range(8):
                        eng = "dve" if c % 2 == 0 else "pool"
                        e = nc.vector if eng == "dve" else nc.gpsimd
                        sy.op(eng, lambda e=e, c=c: e.tensor_tensor(out=xn[:, c, :], in0=h[:, c, :], in1=rstd[:], op=ALU.mult),
                              reads=[hbb, rstdb], writes=[xnb])
                    for f in range(FC):
                        pg = 1 + (rot[0] % 6)
                        rot[0] += 1
                        pu = 1 + (rot[0] % 6)
                        rot[0] += 1
                        for c in range(8):
                            sy.op("pe", lambda c=c, f=f, pg=pg: nc.tensor.matmul(
                                self.ps[pg][:], lhsT=Wg[:, c, f * 128:(f + 1) * 128], rhs=xn[:, c, :],
                                start=(c == 0), stop=(c == 7)), signal=(c == 7), reads=[wgb[c], xnb], writes=[self.b_ps[pg]])
                        for c in range(8):
                            sy.op("pe", lambda c=c, f=f, pu=pu: nc.tensor.matmul(
                                self.ps[pu][:], lhsT=Wu[:, c, f * 128:(f + 1) * 128], rhs=xn[:, c, :],
                                start=(c == 0), stop=(c == 7)), signal=(c == 7), reads=[wub[c], xnb], writes=[self.b_ps[pu]])
                        k = f % 2
                        sy.op("act", lambda pg=pg, k=k: nc.scalar.activation(out=sg[k][:], in_=self.ps[pg][:], func=AF.Silu),
                              reads=[self.b_ps[pg]], writes=[sgb[k]])
                        a3 = ai % 3
                        sy.op("dve", lambda pu=pu, k=k, a3=a3, f=f: nc.vector.tensor_tensor(
                            out=ast[a3][:, f % 2, :], in0=sg[k][:], in1=self.ps[pu][:], op=ALU.mult),
                            reads=[sgb[k], self.b_ps[pu]], writes=[astb[a3]])
                        if f % 2 == 1:
                            sy.dma("pool", self.aT[s, f - 1:f + 1, :, tsl].rearrange("c p t -> p c t"), ast[a3][:], astb[a3],
                                   reads=[astb[a3]], writes=[self.b_aT[s][jb]])
                            ai += 1

    def phase_d(self, l):
        nc, sy = self.nc, self.sy
        with ExitStack() as st:
            stg = [self.sb(st, f"d_stg{i}", [128, 1024], F32) for i in range(3)]
            stgb = [Buf(f"d_stg{i}") for i in range(3)]
            ctr = [0]
            Wd, wdb = self.load_w(st, "d_wd", self.w_down[l], FC, D, None, 1.0, stg, stgb, ctr)
            Wpg, wpgb = self.load_w(st, "d_wpg", self.w_pgate[l], 8, D, self.gain(2, l), 1.0, stg, stgb, ctr)
            Wpp, wppb = self.load_w(st, "d_wpp", self.w_pproj[l], 2, D, None, 1.0, stg, stgb, ctr)
            self.epsc = self.sb(st, "d_eps", [128, 1], F32)
            sy.op("dve", lambda: nc.vector.memset(self.epsc[:], 1e-6), writes=[self.b_const])
            ain = [self.sb(st, f"d_ain{i}", [128, FC, TB], BF16) for i in range(2)]
            aib = [Buf(f"d_ain{i}") for i in range(2)]
            hin = [self.sb(st, f"d_hin{i}", [128, 8, TB], F32) for i in range(2)]
            hb = [Buf(f"d_hin{i}") for i in range(2)]
            pin = [self.sb(st, f"d_pin{i}", [128, 4, PLE], F32) for i in range(2)]
            pib = [Buf(f"d_pin{i}") for i in range(2)]
            pT = self.sb(st, "d_pT", [128, 2, TB], BF16)
            pTb = Buf("d_pT")
            sq = [self.sb(st, f"d_sq{i}", [128, TB], F32) for i in range(2)]
            sqb = [Buf(f"d_sq{i}") for i in range(2)]
            rstd = self.sb(st, "d_rstd", [128, TB], F32)
            rstdb = Buf("d_rstd")
            xn = self.sb(st, "d_xn", [128, 8, TB], BF16)
            xnb = Buf("d_xn")
            gt = [self.sb(st, f"d_gt{i}", [128, TB], F32) for i in range(2)]
            gtb = [Buf(f"d_gt{i}") for i in range(2)]
            rot = [0]
            it = 0
            for s in range(NSEQ):
                for jb in range(NBLK):
                    i2 = it % 2
                    it += 1
                    tsl = slice(jb * TB, (jb + 1) * TB)
                    h, hbb = hin[i2], hb[i2]
                    sy.dma("sp", ain[i2][:], self.aT[s, :, :, tsl].rearrange("c p t -> p c t"), aib[i2],
                           reads=[self.b_aT[s][jb]], writes=[aib[i2]])
                    sy.dma("sp", h[:], self.hT[s, :, :, tsl].rearrange("c p t -> p c t"), hbb,
                           reads=[self.b_hT[s][jb]], writes=[hbb])
                    sy.dma("sp", pin[i2][:], self.p[l, s, tsl, :].rearrange("(a p) d -> p a d", p=128), pib[i2],
                           writes=[pib[i2]])
                    for n in range(8):
                        pi = 1 + (rot[0] % 6)
                        rot[0] += 1
                        for f in range(FC):
                            sy.op("pe", lambda f=f, n=n, pi=pi: nc.tensor.matmul(
                                self.ps[pi][:], lhsT=Wd[:, f, n * 128:(n + 1) * 128], rhs=ain[i2][:, f, :],
                                start=(f == 0), stop=(f == FC - 1)), signal=(f == FC - 1), reads=[wdb[f], aib[i2]], writes=[self.b_ps[pi]])
                        sy.op("dve", lambda n=n, pi=pi: nc.vector.tensor_tensor(out=h[:, n, :], in0=h[:, n, :],
                                                                                in1=self.ps[pi][:], op=ALU.add),
                              reads=[self.b_ps[pi]], writes=[hbb])
                    self.rms_block(h, hbb, sq, sqb, rstd, rstdb, 0)
                    for c in range(8):
                        eng = "dve" if c % 2 == 0 else "pool"
                        e = nc.vector if eng == "dve" else nc.gpsimd
                        sy.op(eng, lambda e=e, c=c: e.tensor_tensor(out=xn[:, c, :], in0=h[:, c, :], in1=rstd[:], op=ALU.mult),
                              reads=[hbb, rstdb], writes=[xnb])
                    for kc in range(2):
                        pi = 1 + (rot[0] % 6)
                        rot[0] += 1
                        for a in range(4):
                            sy.op("pe", lambda a=a, kc=kc, pi=pi: nc.tensor.transpose(
                                self.ps[pi][:, a * 128:(a + 1) * 128], pin[i2][:, a, kc * 128:(kc + 1) * 128],
                                self.ident_f[:]), signal=(a == 3), reads=[pib[i2], self.b_const], writes=[self.b_ps[pi]])
                        sy.op("act", lambda kc=kc, pi=pi: nc.scalar.copy(pT[:, kc, :], self.ps[pi][:]),
                              reads=[self.b_ps[pi]], writes=[pTb])
                    for n in range(8):
                        pg = 1 + (rot[0] % 6)
                        rot[0] += 1
                        pp = 1 + (rot[0] % 6)
                        rot[0] += 1
                        for c in range(8):
                            sy.op("pe", lambda c=c, n=n, pg=pg: nc.tensor.matmul(
                                self.ps[pg][:], lhsT=Wpg[:, c, n * 128:(n + 1) * 128], rhs=xn[:, c, :],
                                start=(c == 0), stop=(c == 7)), signal=(c == 7), reads=[wpgb[c], xnb], writes=[self.b_ps[pg]])
                        for kc in range(2):
                            sy.op("pe", lambda kc=kc, n=n, pp=pp: nc.tensor.matmul(
                                self.ps[pp][:], lhsT=Wpp[:, kc, n * 128:(n + 1) * 128], rhs=pT[:, kc, :],
                                start=(kc == 0), stop=(kc == 1)), signal=(kc == 1), reads=[wppb[kc], pTb], writes=[self.b_ps[pp]])
                        k = n % 2
                        sy.op("act", lambda pg=pg, k=k: nc.scalar.activation(out=gt[k][:], in_=self.ps[pg][:], func=AF.Sigmoid),
                              reads=[self.b_ps[pg]], writes=[gtb[k]])
                        sy.op("dve", lambda pp=pp, k=k: nc.vector.tensor_tensor(out=gt[k][:], in0=gt[k][:], in1=self.ps[pp][:],
                                                                                op=ALU.mult),
                              reads=[self.b_ps[pp], gtb[k]], writes=[gtb[k]])
                        sy.op("pool", lambda n=n, k=k: nc.gpsimd.tensor_tensor(out=h[:, n, :], in0=h[:, n, :], in1=gt[k][:],
                                                                               op=ALU.add),
                              reads=[gtb[k]], writes=[hbb])
                    sy.dma("pool", self.hT[s, :, :, tsl].rearrange("c p t -> p c t"), h[:], hbb,
                           reads=[hbb], writes=[self.b_hT[s][jb]])

    def phase_final(self):
        nc, sy = self.nc, self.sy
        with ExitStack() as st:
            self.epsc = self.sb(st, "f_eps", [128, 1], F32)
            sy.op("dve", lambda: nc.vector.memset(self.epsc[:], 1e-6), writes=[self.b_const])
            hin = [self.sb(st, f"f_hin{i}", [128, 8, TB], F32) for i in range(2)]
            hb = [Buf(f"f_hin{i}") for i in range(2)]
            sq = [self.sb(st, f"f_sq{i}", [128, TB], F32) for i in range(2)]
            sqb = [Buf(f"f_sq{i}") for i in range(2)]
            rstd = self.sb(st, "f_rstd", [128, TB], F32)
            rstdb = Buf("f_rstd")
            yn = self.sb(st, "f_yn", [128, 8, TB], F32)
            ynb = Buf("f_yn")
            ost = [self.sb(st, f"f_ost{i}", [128, 4, D], F32) for i in range(2)]
            ostb = [Buf(f"f_ost{i}") for i in range(2)]
            g = self.gains[:, 12, :]
            rot = [0]
            it = 0
            for s in range(NSEQ):
                for jb in range(NBLK):
                    i2 = it % 2
                    it += 1
                    tsl = slice(jb * TB, (jb + 1) * TB)
                    h, hbb = hin[i2], hb[i2]
                    sy.dma("sp", h[:], self.hT[s, :, :, tsl].rearrange("c p t -> p c t"), hbb,
                           reads=[self.b_hT[s][jb]], writes=[hbb])
                    self.rms_block(h, hbb, sq, sqb, rstd, rstdb, 0)
                    for c in range(8):
                        sy.op("dve", lambda c=c: nc.vector.scalar_tensor_tensor(
                            out=yn[:, c, :], in0=h[:, c, :], scalar=g[:, c:c + 1], in1=rstd[:], op0=ALU.mult, op1=ALU.mult),
                            reads=[hbb, rstdb, self.b_const], writes=[ynb])
                    for a in range(4):
                        for cg in range(2):
                            pi = 1 + (rot[0] % 6)
                            rot[0] += 1
                            for cc in range(4):
                                c = cg * 4 + cc
                                sy.op("pe", lambda a=a, c=c, cc=cc, pi=pi: nc.tensor.transpose(
                                    self.ps[pi][:, cc * 128:(cc + 1) * 128], yn[:, c, a * 128:(a + 1) * 128], self.ident_f[:]),
                                    signal=(cc == 3), reads=[ynb, self.b_const], writes=[self.b_ps[pi]])
                            if cg == 0:
                                sy.op("act", lambda a=a, cg=cg, pi=pi: nc.scalar.copy(ost[i2][:, a, cg * 512:(cg + 1) * 512],
                                                                                      self.ps[pi][:]),
                                      reads=[self.b_ps[pi]], writes=[ostb[i2]])
                            else:
                                sy.op("dve", lambda a=a, cg=cg, pi=pi: nc.vector.tensor_copy(
                                    ost[i2][:, a, cg * 512:(cg + 1) * 512], self.ps[pi][:]),
                                    reads=[self.b_ps[pi]], writes=[ostb[i2]])
                    sy.dma("pool", self.out[s, tsl, :].rearrange("(a p) d -> p a d", p=128), ost[i2][:], ostb[i2],
                           reads=[ostb[i2]], writes=[Buf("out")])


_CONSTS = None


def make_in_maps(inputs):
    global _CONSTS
    if _CONSTS is None:
        _CONSTS = _const_tables()
    qaug, kaug, cbase = _CONSTS
    f = lambda k: np.ascontiguousarray(np.asarray(inputs[k], dtype=np.float32))
    x = f("x")
    p = f("p")
    shared = {k: f(k) for k in ["norm_mix", "norm_ffn", "norm_ple", "norm_final", "da_w_qkv", "da_w_o",
                                "da_lambda_q1", "da_lambda_k1", "da_lambda_q2", "da_lambda_k2", "da_subln",
                                "na_w_qkv", "na_b_qkv", "na_w_o", "ffn_w_gate", "ffn_w_up", "ffn_w_down",
                                "ple_w_proj", "ple_w_gate"]}
    rpb = f("na_rpb")
    shared["na_tab"] = np.stack([_na_bias_table(rpb[j]).reshape(16, 64, 15 * 64) for j in range(2)])
    shared["c_ident"] = np.eye(128, dtype=np.float32)
    shared["c_qaug"] = qaug
    shared["c_kaug"] = kaug
    shared["c_cbase"] = cbase
    maps = []
    for i in range(NCORES):
        m = dict(shared)
        m["x"] = np.ascontiguousarray(x[2 * i:2 * i + 2])
        m["p"] = np.ascontiguousarray(p[:, 2 * i:2 * i + 2])
        maps.append(m)
    return maps


def kernel(**inputs):
    prog = Prog()
    nc = prog.build()
    maps = make_in_maps(inputs)
    res = run_bass_kernel_spmd(nc, maps, core_ids=list(range(NCORES)))
    return np.concatenate([np.asarray(r["out"]) for r in res.results], axis=0).astype(np.float32)
```
